# Optimizing a Trainium2 kernel written in Bass

```python
import math
import jax, jax.numpy as jnp
from jax import lax
import numpy as np

D_MODEL = 2048
BATCH = 1
SEQ = 8192
DEPTH = 2

N_HEADS = 8
HEAD_DIM = 128
ATTN_W = N_HEADS * HEAD_DIM
CONV_CH = D_MODEL // 2
CONV_K = 31
FFN_HIDDEN = ((8 * D_MODEL // 3 + 255) // 256) * 256
PLE_DIM = 256
Q_BLOCK = 128
EPS = 1e-6
IN_COLS = 2 * CONV_CH + 3 * ATTN_W + N_HEADS + 2 * D_MODEL
NEG_INF = -1e30

kernel_name = "hybrid_conformer_fox_gated_trunk"


def rmsnorm(x, g):
    xf = x.astype(jnp.float32)
    y = xf * lax.rsqrt(jnp.mean(xf * xf, axis=-1, keepdims=True) + EPS)
    return (y * g.astype(jnp.float32)).astype(x.dtype)


def layernorm(x, g, b):
    xf = x.astype(jnp.float32)
    mu = jnp.mean(xf, axis=-1, keepdims=True)
    var = jnp.mean(jnp.square(xf - mu), axis=-1, keepdims=True)
    y = (xf - mu) * lax.rsqrt(var + EPS)
    return (y * g.astype(jnp.float32) + b.astype(jnp.float32)).astype(x.dtype)


def conformer_conv(glu_in, conv_w, conv_b, ln_g, ln_b, w_conv_out):
    a, gate = jnp.split(glu_in, 2, axis=-1)
    u = a * jax.nn.sigmoid(gate)
    u = lax.conv_general_dilated(
        u, conv_w[:, None, :].astype(u.dtype),
        window_strides=(1,), padding=[(CONV_K - 1, 0)],
        dimension_numbers=("NWC", "WIO", "NWC"),
        feature_group_count=CONV_CH) + conv_b
    u = jax.nn.silu(layernorm(u, ln_g, ln_b))
    return u @ w_conv_out


def forgetting_attention(q, k, v, log_f):
    b, s, h, d = q.shape
    nb = s // Q_BLOCK
    scale = 1.0 / math.sqrt(d)
    c = jnp.cumsum(log_f, axis=1).transpose(0, 2, 1)
    q_blocks = q.reshape(b, nb, Q_BLOCK, h, d).transpose(1, 0, 2, 3, 4)
    c_blocks = c.reshape(b, h, nb, Q_BLOCK).transpose(2, 0, 1, 3)
    starts = jnp.arange(nb, dtype=jnp.int32) * Q_BLOCK
    k_pos = jnp.arange(s, dtype=jnp.int32)

    def one_block(args):
        qb, cb, start = args
        sc = jnp.einsum("bqhd,bkhd->bhqk", qb, k,
                        preferred_element_type=jnp.float32) * scale
        sc = sc + cb[..., :, None] - c[:, :, None, :]
        q_pos = start + jnp.arange(Q_BLOCK, dtype=jnp.int32)
        mask = k_pos[None, :] <= q_pos[:, None]
        sc = jnp.where(mask, sc, NEG_INF)
        pr = jax.nn.softmax(sc, axis=-1)
        return jnp.einsum("bhqk,bkhd->bqhd", pr.astype(v.dtype), v)

    out = lax.map(one_block, (q_blocks, c_blocks, starts))
    return out.transpose(1, 0, 2, 3, 4).reshape(b, s, h * d)


def setup_inputs(seed: int = 0) -> dict:
    key = jax.random.key(seed)
    ks = jax.random.split(key, 20)
    f32 = jnp.float32

    def w(k, shape, fan_in):
        return jax.random.normal(k, shape, f32) * (fan_in ** -0.5)

    def gain(k, shape):
        return 1.0 + 0.02 * jax.random.normal(k, shape, f32)

    return {
        "x": jax.random.normal(ks[0], (BATCH, SEQ, D_MODEL), f32),
        "p": jax.random.normal(ks[1], (DEPTH, BATCH, SEQ, PLE_DIM), f32),
        "norm_mix_g": gain(ks[2], (DEPTH, D_MODEL)),
        "w_in": w(ks[3], (DEPTH, D_MODEL, IN_COLS), D_MODEL),
        "b_forget": 2.0 + 0.1 * jax.random.normal(ks[4], (DEPTH, N_HEADS), f32),
        "conv_w": w(ks[5], (DEPTH, CONV_K, CONV_CH), CONV_K),
        "conv_b": 0.02 * jax.random.normal(ks[6], (DEPTH, CONV_CH), f32),
        "conv_ln_g": gain(ks[7], (DEPTH, CONV_CH)),
        "conv_ln_b": 0.02 * jax.random.normal(ks[8], (DEPTH, CONV_CH), f32),
        "w_conv_out": w(ks[9], (DEPTH, CONV_CH, D_MODEL), CONV_CH),
        "w_attn_out": w(ks[10], (DEPTH, ATTN_W, D_MODEL), ATTN_W),
        "w_out": w(ks[11], (DEPTH, D_MODEL, D_MODEL), D_MODEL),
        "norm_ffn_g": gain(ks[12], (DEPTH, D_MODEL)),
        "w_gate_up": w(ks[13], (DEPTH, D_MODEL, 2 * FFN_HIDDEN), D_MODEL),
        "w_down": w(ks[14], (DEPTH, FFN_HIDDEN, D_MODEL), FFN_HIDDEN),
        "norm_ple_g": gain(ks[15], (DEPTH, D_MODEL)),
        "w_ple_gate": w(ks[16], (DEPTH, D_MODEL, D_MODEL), D_MODEL),
        "w_ple_proj": w(ks[17], (DEPTH, PLE_DIM, D_MODEL), PLE_DIM),
        "final_g": gain(ks[18], (D_MODEL,)),
    }


def reference(x, p, norm_mix_g, w_in, b_forget, conv_w, conv_b, conv_ln_g, conv_ln_b,
              w_conv_out, w_attn_out, w_out, norm_ffn_g, w_gate_up, w_down,
              norm_ple_g, w_ple_gate, w_ple_proj, final_g):
    b, s, _ = x.shape
    split_pts = np.cumsum([2 * CONV_CH, ATTN_W, ATTN_W, ATTN_W, N_HEADS, D_MODEL]).tolist()
    for i in range(DEPTH):
        h = rmsnorm(x, norm_mix_g[i])
        proj = h @ w_in[i]
        glu_in, q, k, v, f_logit, g_conv, g_attn = jnp.split(proj, split_pts, axis=-1)

        y_conv = conformer_conv(glu_in, conv_w[i], conv_b[i], conv_ln_g[i],
                                conv_ln_b[i], w_conv_out[i])

        q = q.reshape(b, s, N_HEADS, HEAD_DIM)
        k = k.reshape(b, s, N_HEADS, HEAD_DIM)
        v = v.reshape(b, s, N_HEADS, HEAD_DIM)
        log_f = jax.nn.log_sigmoid((f_logit + b_forget[i]).astype(jnp.float32))
        y_attn = forgetting_attention(q, k, v, log_f) @ w_attn_out[i]

        merged = jax.nn.sigmoid(g_conv) * y_conv + jax.nn.sigmoid(g_attn) * y_attn
        x = x + merged @ w_out[i]

        hf = rmsnorm(x, norm_ffn_g[i])
        gate, up = jnp.split(hf @ w_gate_up[i], 2, axis=-1)
        x = x + (jax.nn.silu(gate) * up) @ w_down[i]

        hp = rmsnorm(x, norm_ple_g[i])
        x = x + jax.nn.sigmoid(hp @ w_ple_gate[i]) * (p[i] @ w_ple_proj[i])
    return rmsnorm(x, final_g)
```

```python
import os
import numpy as np
import ml_dtypes
import concourse.bass as bass
import concourse.mybir as mybir
from concourse.bass_utils import run_bass_kernel_spmd

F32 = mybir.dt.float32
BF16 = mybir.dt.bfloat16
AF = mybir.ActivationFunctionType
ALU = mybir.AluOpType
AX = mybir.AxisListType

NCORES = 8
D = 2048
S = 8192
T = S // NCORES
DEPTH = 2
NH = 8
HD = 128
CC = 1024
CK = 31
FF = 5632
PLE = 256
INC = 9224
EPS = 1e-6
KD = D // 128
HALO = 32
NSLAB = 6
SLAB_E = 4096

C_A, C_G, C_Q, C_K, C_V, C_F, C_GC, C_GA = 0, 1024, 2048, 3072, 4096, 5120, 5128, 7176

V_NMG, V_NFG, V_NPG = 0, 16, 32
V_CB, V_CLG, V_CLB = 48, 56, 64
V_CW = 72
V_BF = V_CW + CK * 8
V_LAYER = V_BF + 1
V_FIN = DEPTH * V_LAYER
V_EPS = V_FIN + 16
NV = V_EPS + 1

CF_ID, CF_MD, CF_MC, CF_ONE, CF_TRIU, CF_STRICT, CF_INCL = range(7)
CB_ID, CB_ONE, CB_MASK = range(3)


class Res:
    __slots__ = ("name", "w", "rs")

    def __init__(self, name=""):
        self.name = name
        self.w = None
        self.rs = []


class Op:
    __slots__ = ("eng", "idx", "fn", "waits", "dma", "signal", "sem", "val", "inc")


ENGS = ("pe", "act", "dve", "pool", "sp")
DMA_RING = {"pool": 6, "sp": 4, "pe": 1, "act": 1, "dve": 1}


class Plan:
    def __init__(self):
        self.streams = {e: [] for e in ENGS}
        self.known = {e: {} for e in ENGS}
        self.known_dma = {e: set() for e in ENGS}
        self.ring = {e: [None] * DMA_RING[e] for e in ENGS}
        self.ring_uses = {e: [0] * DMA_RING[e] for e in ENGS}
        self.ndma = {e: 0 for e in ENGS}
        self.ccuses = {}

    def add(self, eng, fn, reads=(), writes=(), dma=False, acc=False, cc=None):
        op = Op()
        op.eng, op.fn, op.dma, op.signal, op.sem, op.val, op.inc = eng, fn, (dma or cc is not None), False, None, 0, 1
        deps = []
        for r in reads:
            if r.w is not None:
                deps.append(r.w)
        for w in writes:
            if w.w is not None:
                if not (acc and eng == "pe" and w.w.eng == "pe" and not w.w.dma):
                    deps.append(w.w)
            deps.extend(w.rs)
        if dma:
            slot = self.ndma[eng] % DMA_RING[eng]
            self.ndma[eng] += 1
            prev = self.ring[eng][slot]
            if prev is not None:
                deps.append(prev)
            self.ring[eng][slot] = op
            self.ring_uses[eng][slot] += 1
            op.sem = ("ring", eng, slot)
            op.val = 16 * self.ring_uses[eng][slot]
            op.inc = 16
            op.signal = True
        if cc is not None:
            op.sem = ("cc", cc)
            self.ccuses[cc] = self.ccuses.get(cc, 0) + 1
            op.val = self.ccuses[cc]
            op.inc = 1
            op.signal = True
        op.waits = self._filter(eng, deps)
        for d in op.waits:
            d.signal = True
        op.idx = len(self.streams[eng])
        self.streams[eng].append(op)
        for r in reads:
            r.rs.append(op)
        for w in writes:
            w.w = op
            w.rs = []
        return op

    def _filter(self, eng, deps):
        best = {}
        out = []
        kd = self.known_dma[eng]
        for d in deps:
            if d.dma:
                if d not in kd:
                    kd.add(d)
                    out.append(d)
            else:
                b = best.get(d.eng)
                if b is None or d.idx > b.idx:
                    best[d.eng] = d
        kn = self.known[eng]
        for p, d in best.items():
            if kn.get(p, -1) < d.idx:
                kn[p] = d.idx
                out.append(d)
        return out

    def emit(self, nc, pre=None):
        sems = {}
        names = set()
        for e in ENGS:
            cnt = 0
            for op in self.streams[e]:
                if op.dma:
                    names.add(op.sem)
                elif op.signal:
                    cnt += 1
                    op.sem = ("eng", e)
                    op.val = cnt
                    names.add(op.sem)
        import contextlib
        with contextlib.ExitStack() as st:
            for n in sorted(names, key=str):
                sems[n] = st.enter_context(nc.semaphore("s_" + "_".join(str(x) for x in n)))
            self.sem_nums = sorted(h.num for h in sems.values())
            with nc.Block() as block:
                def run(e, handle):
                    if pre is not None:
                        pre(e, handle)
                    for op in self.streams[e]:
                        for d in op.waits:
                            handle.wait_ge(sems[d.sem], d.val)
                        ins = op.fn(handle)
                        if op.signal:
                            ins.then_inc(sems[op.sem], op.inc)
                    last = {}
                    for op in self.streams[e]:
                        if op.dma:
                            last[op.sem] = max(last.get(op.sem, 0), op.val)
                    for sname, v in last.items():
                        handle.wait_ge(sems[sname], v)

                @block.tensor
                def _(h):
                    run("pe", h)

                @block.scalar
                def _(h):
                    run("act", h)

                @block.vector
                def _(h):
                    run("dve", h)

                @block.gpsimd
                def _(h):
                    run("pool", h)

                @block.sync
                def _(h):
                    run("sp", h)


SB_BASE = 16512
SB_TOP = 229344
SLOT = 512
SCALE = 1.0 / float(np.sqrt(HD))
STAGES = ("loadx", "norm0", "projA", "attn", "mix", "ffn", "layer0", "l0final", "l1only", "full")


class Builder:
    def __init__(self, stage="full", dbg=None):
        self.stage = stage
        self.si = STAGES.index(stage)
        self.dbg = dbg
        self.nc = bass.Bass("TRN2", target_bir_lowering=False)
        self.pl = Plan()
        self.rd = {}
        self.pid = {}
        self._psum_i = 0
        self.bases = {}
        self.slots = [Res(f"sb{i}") for i in range((SB_TOP - SB_BASE) // SLOT + 2)]
        self._n = 0

    def R(self, *key):
        r = self.rd.get(key)
        if r is None:
            r = self.rd[key] = Res(str(key))
        return r

    def A(self, *aps):
        out = []
        for ap in aps:
            base, esz = self.bases[ap.tensor.name]
            pat = ap.ap
            off = int(ap.offset) % int(pat[0][0])
            ext = 1
            for st, cnt in pat[1:]:
                ext += (cnt - 1) * abs(st)
            lo = base + off * esz
            hi = lo + ext * esz
            for sidx in range((lo - SB_BASE) // SLOT, (hi - 1 - SB_BASE) // SLOT + 1):
                out.append(self.slots[sidx])
        return out

    def sb(self, name, shape, dt, off):
        esz = 4 if dt == F32 else 2
        nbytes = int(np.prod(shape[1:])) * esz
        assert off % 32 == 0, (name, off)
        assert SB_BASE <= off and off + nbytes <= SB_TOP, (name, off, nbytes)
        self._n += 1
        t = self.nc.alloc_sbuf_tensor_at(f"{name}_{self._n}", list(shape), dt, offset=off)
        ap = t.ap()
        self.bases[ap.tensor.name] = (off, esz)
        return ap

    def declare(self):
        nc = self.nc
        di = lambda n, s, d=F32: nc.dram_tensor(n, s, d, kind="ExternalInput").ap()
        self.x_in = di("x", [T, D])
        self.vecs_in = di("vecs", [128, NV])
        self.cf_in = di("cf", [128, 7, 128])
        self.cb_in = di("cb", [128, 3, 128], BF16)
        if self.si >= STAGES.index("projA"):
            self.w_in = di("w_in", [DEPTH, D, INC])
        if self.si >= STAGES.index("mix"):
            self.w_co = di("w_conv_out", [DEPTH, CC, D])
            self.w_ao = di("w_attn_out", [DEPTH, CC, D])
            self.w_o = di("w_out", [DEPTH, D, D])
        if self.si >= STAGES.index("ffn"):
            self.w_gu = di("w_gate_up", [DEPTH, D, 2 * FF])
            self.w_dn = di("w_down", [DEPTH, FF, D])
        if self.si >= STAGES.index("layer0"):
            self.p_in = di("p", [DEPTH, T, PLE])
            self.w_pg = di("w_ple_gate", [DEPTH, D, D])
            self.w_pp = di("w_ple_proj", [DEPTH, PLE, D])
        self.out = nc.dram_tensor("out", [T, D], F32, kind="ExternalOutput").ap()
        self.sndq = nc.dram_tensor("sndq", [NH * HD, T], BF16).ap()
        self.gatq = nc.dram_tensor("gatq", [NCORES * NH * HD, T], BF16).ap()
        self.sndk = nc.dram_tensor("sndk", [NH * HD, T], BF16).ap()
        self.gatk = nc.dram_tensor("gatk", [NCORES * NH * HD, T], BF16).ap()
        self.sndv = nc.dram_tensor("sndv", [NH * T, HD], BF16).ap()
        self.gatv = nc.dram_tensor("gatv", [NCORES * NH * T, HD], BF16).ap()
        self.sndf = nc.dram_tensor("sndf", [NH * 8, 128], F32).ap()
        self.gatf = nc.dram_tensor("gatf", [NCORES * NH * 8, 128], F32).ap()
        self.sndh = nc.dram_tensor("sndh", [CC, HALO], F32).ap()
        self.gath = nc.dram_tensor("gath", [(NCORES + 1) * CC, HALO], F32).ap()
        self.snd2 = nc.dram_tensor("snd2", [HD, S], BF16).ap()
        self.gat2 = nc.dram_tensor("gat2", [NH * HD, S], BF16).ap()
        if self.dbg is not None:
            name, shape, dt = self.dbg
            self.dbg_out = nc.dram_tensor("dbg", list(shape), dt, kind="ExternalOutput").ap()

        o = SB_BASE
        self.o_xT = o; o += KD * T * 4
        self.o_hT = o; o += KD * T * 2
        self.o_r1 = o; o += 8 * (HALO + T) * 4
        self.o_cn = o; o += 8 * T * 2
        self.o_w = o; o += NSLAB * SLAB_E * 2
        self.o_vec = o; o += (NV * 4 + 31) // 32 * 32
        self.o_cf = o; o += 7 * 128 * 4
        self.o_cb = o; o += 3 * 128 * 2
        self.o_misc = o
        assert SB_TOP - o >= 8192, SB_TOP - o
        self.xT = self.sb("xT", [128, KD, T], F32, self.o_xT)
        self.hT = self.sb("hT", [128, KD, T], BF16, self.o_hT)
        self.uT = self.sb("uT", [128, 8, HALO + T], F32, self.o_r1)
        self.cnT = self.sb("cnT", [128, 8, T], BF16, self.o_cn)
        self.wsl = [self.sb(f"wsl{i}", [128, SLAB_E], BF16, self.o_w + i * SLAB_E * 2) for i in range(NSLAB)]
        self.wnext = 0
        self.vecs = self.sb("vecs", [128, NV], F32, self.o_vec)
        self.cf = self.sb("cf", [128, 7, 128], F32, self.o_cf)
        self.cb = self.sb("cb", [128, 3, 128], BF16, self.o_cb)
        self.ps = [nc.alloc_psum_tensor(f"ps{i}", [128, 512], F32).ap() for i in range(8)]

    def psum(self):
        i = self._psum_i % 8
        self._psum_i += 1
        return i

    def P(self, b):
        return self.R("ps", b)

    def misc(self, name, shape, dt, off):
        return self.sb(name, shape, dt, self.o_misc + off)

    def load_consts(self):
        pl = self.pl
        pl.add("sp", lambda e: e.dma_start(out=self.vecs, in_=self.vecs_in), writes=[self.R("vecs")], dma=True)
        pl.add("sp", lambda e: e.dma_start(out=self.cf, in_=self.cf_in), writes=[self.R("cf")], dma=True)
        pl.add("sp", lambda e: e.dma_start(out=self.cb, in_=self.cb_in), writes=[self.R("cb")], dma=True)

    def wload(self, src_ap, kchunks, ncols, slot_off=0, slot=None):
        assert slot_off + kchunks * ncols <= SLAB_E
        if slot is None:
            slot = self.wnext % NSLAB
            self.wnext += 1
        view = self.wsl[slot][:, slot_off:slot_off + kchunks * ncols].rearrange("p (k c) -> p k c", k=kchunks)
        src = src_ap.rearrange("(k p) c -> p k c", p=128)
        self.pl.add("pool", lambda e: e.dma_start(out=view, in_=src), writes=self.A(view), dma=True)
        return view

    def evac(self, i, dst, b, src=None):
        src = self.ps[b] if src is None else src
        if i % 2 == 0:
            self.pl.add("act", lambda e: e.copy(dst, src), reads=[self.P(b)], writes=self.A(dst))
        else:
            self.pl.add("dve", lambda e: e.tensor_copy(dst, src), reads=[self.P(b)], writes=self.A(dst))

    def mm(self, b, out, lhsT, rhs, first, last, extra_reads=()):
        self.pl.add("pe", lambda e: e.matmul(out, lhsT, rhs, start=first, stop=last),
                    reads=self.A(lhsT, rhs) + list(extra_reads), writes=[self.P(b)], acc=(not first))

    def phase_load_x(self):
        pl = self.pl
        ident = self.cf[:, CF_ID, :]
        stg = [self.sb(f"xstg{i}", [128, D], F32, self.o_r1 + i * 8192) for i in range(2)]
        n = 0
        for tt in range(T // 128):
            sl = tt % 2
            pl.add("sp", lambda e, sl=sl, tt=tt: e.dma_start(out=stg[sl], in_=self.x_in[tt * 128:(tt + 1) * 128, :]),
                   writes=self.A(stg[sl]), dma=True)
            for kg in range(KD // 4):
                b = self.psum()
                for kk in range(4):
                    k = kg * 4 + kk
                    src = stg[sl][:, k * 128:(k + 1) * 128]
                    pl.add("pe", lambda e, b=b, kk=kk, src=src: e.transpose(self.ps[b][:, kk * 128:(kk + 1) * 128], src, ident),
                           reads=self.A(src) + [self.R("cf")], writes=[self.P(b)], acc=(kk > 0))
                for kk in range(4):
                    dst = self.xT[:, kg * 4 + kk, tt * 128:(tt + 1) * 128]
                    self.evac(n, dst, b, self.ps[b][:, kk * 128:(kk + 1) * 128])
                n += 1

    def phase_rmsnorm(self, gcol0):
        pl = self.pl
        self._n += 1
        sq = self.misc("sq", [128, 2, 512], F32, 0)
        rstd = self.misc("rstd", [128, T], F32, 4096)
        md = self.cf[:, CF_MD, :]
        epsc = self.vecs[:, V_EPS:V_EPS + 1]
        for half in range(2):
            ts = slice(half * 512, (half + 1) * 512)
            b = self.psum()
            for k in range(KD):
                sl = k % 2
                src = self.xT[:, k, ts]
                pl.add("act", lambda e, src=src, sl=sl: e.activation(sq[:, sl, :], src, AF.Square),
                       reads=self.A(src), writes=self.A(sq[:, sl, :]))
                self.mm(b, self.ps[b], md, sq[:, sl, :], k == 0, k == KD - 1, [self.R("cf")])
            r = rstd[:, ts]
            pl.add("act", lambda e, b=b, r=r: e.activation(r, self.ps[b], AF.Sqrt, bias=epsc),
                   reads=[self.P(b), self.R("vecs")], writes=self.A(r))
            pl.add("dve", lambda e, r=r: e.reciprocal(r, r), reads=self.A(r), writes=self.A(r))
            for k in range(KD):
                g = self.vecs[:, gcol0 + k:gcol0 + k + 1]
                src = self.xT[:, k, ts]
                dst = self.hT[:, k, ts]
                pl.add("dve", lambda e, src=src, dst=dst, g=g, r=r: e.scalar_tensor_tensor(dst, src, g, r, ALU.mult, ALU.mult),
                       reads=self.A(src, r) + [self.R("vecs")], writes=self.A(dst))

    def proj_T(self, wv, ncolblk, rhsT, kchunks, consume):
        for cb_ in range(ncolblk):
            for half in range(2):
                b = self.psum()
                ts = slice(half * 512, (half + 1) * 512)
                for k in range(kchunks):
                    self.mm(b, self.ps[b], wv[:, k, cb_ * 128:(cb_ + 1) * 128], rhsT[:, k, ts], k == 0, k == kchunks - 1)
                consume(cb_, half, b)

    def phase_qkvf(self, L):
        pl = self.pl
        win = self.w_in[L]
        stq = [self.sb(f"stq{i}", [128, T], BF16, self.o_cn + i * 2048) for i in range(2)]
        stv = [self.sb(f"stv{i}", [128, 256], BF16, self.o_cn + 4096 + i * 512) for i in range(2)]
        wf = self.sb("wf", [128, KD, NH], BF16, self.o_cn + 5120)
        lf = self.sb("lf", [NH, T], F32, self.o_cn + 5376)
        ev = [0]
        sq_, sk_, sv_ = [], [], []
        for which, cbase in ((0, C_Q), (1, C_K)):
            for sl in range(4):
                wv = self.wload(win[:, cbase + sl * 256: cbase + (sl + 1) * 256], KD, 256)

                def consume(cb_, half, b, sl=sl, which=which):
                    h = sl * 2 + cb_
                    st = h % 2
                    self.evac(ev[0], stq[st][:, half * 512:(half + 1) * 512], b)
                    ev[0] += 1
                    if half == 1:
                        dst = (self.sndq if which == 0 else self.sndk)[h * HD:(h + 1) * HD, :]
                        r = self.R("snd1", h, which)
                        (sq_ if which == 0 else sk_).append(r)
                        pl.add("sp", lambda e, dst=dst, st=st: e.dma_start(out=dst, in_=stq[st]),
                               reads=self.A(stq[st]), writes=[r], dma=True)
                self.proj_T(wv, 2, self.hT, KD, consume)
        sview = self.sndv.rearrange("(h t) d -> t h d", h=NH)
        for sl in range(4):
            wv = self.wload(win[:, C_V + sl * 256: C_V + (sl + 1) * 256], KD, 256)
            for tt in range(T // 128):
                b = self.psum()
                st = tt % 2
                for k in range(KD):
                    self.mm(b, self.ps[b][:, 0:256], self.hT[:, k, tt * 128:(tt + 1) * 128], wv[:, k, :], k == 0, k == KD - 1)
                self.evac(ev[0], stv[st], b, self.ps[b][:, 0:256])
                ev[0] += 1
                dst = sview[tt * 128:(tt + 1) * 128, sl * 2:sl * 2 + 2, :]
                src = stv[st].rearrange("p (h d) -> p h d", h=2)
                r = self.R("snd1", "v", sl, tt)
                sv_.append(r)
                pl.add("sp", lambda e, dst=dst, src=src: e.dma_start(out=dst, in_=src),
                       reads=self.A(stv[st]), writes=[r], dma=True)
        wsrc = win[:, C_F:C_F + NH].rearrange("(k p) c -> p k c", p=128)
        pl.add("pool", lambda e: e.dma_start(out=wf, in_=wsrc), writes=self.A(wf), dma=True)
        bcol = self.vecs[0:NH, L * V_LAYER + V_BF: L * V_LAYER + V_BF + 1]
        for half in range(2):
            b = self.psum()
            ts = slice(half * 512, (half + 1) * 512)
            for k in range(KD):
                self.mm(b, self.ps[b][0:NH, :], wf[:, k, :], self.hT[:, k, ts], k == 0, k == KD - 1)
            dst = lf[:, ts]
            pl.add("act", lambda e, b=b, dst=dst: e.activation(dst, self.ps[b][0:NH, :], AF.Sigmoid, bias=bcol),
                   reads=[self.P(b), self.R("vecs")], writes=self.A(dst))
            pl.add("act", lambda e, dst=dst: e.activation(dst, dst, AF.Ln), reads=self.A(dst), writes=self.A(dst))
        sfv = self.sndf.rearrange("(h b) p -> h (b p)", h=NH)
        pl.add("sp", lambda e: e.dma_start(out=sfv, in_=lf), reads=self.A(lf), writes=[self.R("sndf")], dma=True)
        rg = [list(range(NCORES))]
        for (snd, gat, rs, nm) in ((self.sndk, self.gatk, sk_, "gatk"), (self.sndv, self.gatv, sv_, "gatv"),
                                   (self.sndq, self.gatq, sq_, "gatq")):
            pl.add("pool", lambda e, snd=snd, gat=gat: e.collective_compute("AllGather", ALU.bypass, replica_groups=rg,
                                                                            ins=[snd], outs=[gat]),
                   reads=rs, writes=[self.R(nm)], cc=nm)
        pl.add("pool", lambda e: e.collective_compute("AllGather", ALU.bypass, replica_groups=rg,
                                                      ins=[self.sndf], outs=[self.gatf]),
               reads=[self.R("sndf")], writes=[self.R("gatf")], cc="gatf")

    def phase_glu(self, L):
        pl = self.pl
        win = self.w_in[L]
        sig = [self.misc(f"sig{i}", [128, 512], F32, i * 2048) for i in range(2)]
        n = 0
        for sl in range(4):
            wa = self.wload(win[:, C_A + sl * 256: C_A + (sl + 1) * 256], KD, 256)
            wg = self.wload(win[:, C_G + sl * 256: C_G + (sl + 1) * 256], KD, 256)
            for jb in range(2):
                j = sl * 2 + jb
                for half in range(2):
                    ba, bg = self.psum(), self.psum()
                    ts = slice(half * 512, (half + 1) * 512)
                    for (b, wv) in ((bg, wg), (ba, wa)):
                        for k in range(KD):
                            self.mm(b, self.ps[b], wv[:, k, jb * 128:(jb + 1) * 128], self.hT[:, k, ts], k == 0, k == KD - 1)
                    sg = sig[n % 2]
                    n += 1
                    pl.add("act", lambda e, bg=bg, sg=sg: e.activation(sg, self.ps[bg], AF.Sigmoid),
                           reads=[self.P(bg)], writes=self.A(sg))
                    dst = self.uT[:, j, HALO + half * 512: HALO + (half + 1) * 512]
                    pl.add("dve", lambda e, ba=ba, sg=sg, dst=dst: e.tensor_tensor(dst, self.ps[ba], sg, ALU.mult),
                           reads=[self.P(ba)] + self.A(sg), writes=self.A(dst))

    def phase_halo(self, L):
        pl = self.pl
        rg = [list(range(NCORES))]
        if L == 0:
            zt = self.misc("zt", [128, 8, HALO], F32, 4096)
            pl.add("dve", lambda e: e.memset(zt, 0.0), writes=self.A(zt))
            dstz = self.gath[0:CC, :].rearrange("(j p) t -> p j t", p=128)
            pl.add("sp", lambda e: e.dma_start(out=dstz, in_=zt), reads=self.A(zt), writes=[self.R("gathz")], dma=True)
        dst = self.sndh.rearrange("(j p) t -> p j t", p=128)
        src = self.uT[:, :, T:T + HALO]
        pl.add("sp", lambda e: e.dma_start(out=dst, in_=src), reads=self.A(src),
               writes=[self.R("sndh")], dma=True)
        pl.add("pool", lambda e: e.collective_compute("AllGather", ALU.bypass, replica_groups=rg,
                                                      ins=[self.sndh], outs=[self.gath[CC:(NCORES + 1) * CC, :]]),
               reads=[self.R("sndh")], writes=[self.R("gath")], cc="gath")
        hv = self.uT[:, :, 0:HALO]

        def ld(e):
            pid = self.pid["v"]
            srcd = self.gath[bass.ds(pid * CC, CC), :].rearrange("(j p) t -> p j t", p=128)
            return e.dma_start(out=hv, in_=srcd)
        wr = []
        for j in range(8):
            wr += self.A(self.uT[:, j, 0:HALO])
        pl.add("pool", ld, reads=[self.R("gath"), self.R("gathz")], writes=wr, dma=True)

    def phase_conv(self, L):
        pl = self.pl
        acc = [self.misc(f"cacc{i}", [128, T], F32, i * 4096) for i in range(2)]
        vb = L * V_LAYER
        for jp in range(4):
            for k in range(CK):
                for jj in range(2):
                    j = jp * 2 + jj
                    wcol = self.vecs[:, vb + V_CW + k * 8 + j: vb + V_CW + k * 8 + j + 1]
                    src = self.uT[:, j, 2 + k: 2 + k + T]
                    a = acc[jj]
                    if k == 0:
                        bcol = self.vecs[:, vb + V_CB + j: vb + V_CB + j + 1]
                        pl.add("dve", lambda e, a=a, src=src, wcol=wcol, bcol=bcol: e.tensor_scalar(a, src, wcol, bcol, ALU.mult, ALU.add),
                               reads=self.A(src) + [self.R("vecs")], writes=self.A(a))
                    elif k < CK - 1:
                        pl.add("dve", lambda e, a=a, src=src, wcol=wcol: e.scalar_tensor_tensor(a, src, wcol, a, ALU.mult, ALU.add),
                               reads=self.A(src, a), writes=self.A(a))
                    else:
                        dst = self.uT[:, j, HALO:HALO + T]
                        pl.add("dve", lambda e, a=a, src=src, wcol=wcol, dst=dst: e.scalar_tensor_tensor(dst, src, wcol, a, ALU.mult, ALU.add),
                               reads=self.A(src, a), writes=self.A(dst))
        mc = self.cf[:, CF_MC, :]
        sq = self.misc("lnsq", [128, 512], F32, 0)
        meanS = self.misc("lnmean", [128, 512], F32, 2048)
        rstdS = self.misc("lnrstd", [128, 512], F32, 4096)
        tmp = self.misc("lntmp", [128, 512], F32, 6144)
        epsc = self.vecs[:, V_EPS:V_EPS + 1]
        for half in range(2):
            ts = slice(half * 512, (half + 1) * 512)
            us = slice(HALO + half * 512, HALO + (half + 1) * 512)
            bm, bq = self.psum(), self.psum()
            for j in range(8):
                self.mm(bm, self.ps[bm], mc, self.uT[:, j, us], j == 0, j == 7, [self.R("cf")])
            for j in range(8):
                src = self.uT[:, j, us]
                pl.add("act", lambda e, src=src: e.activation(sq, src, AF.Square), reads=self.A(src), writes=self.A(sq))
                self.mm(bq, self.ps[bq], mc, sq, j == 0, j == 7, [self.R("cf")])
            pl.add("act", lambda e, bm=bm: e.copy(meanS, self.ps[bm]), reads=[self.P(bm)], writes=self.A(meanS))
            pl.add("dve", lambda e: e.tensor_tensor(rstdS, meanS, meanS, ALU.mult), reads=self.A(meanS), writes=self.A(rstdS))
            pl.add("dve", lambda e, bq=bq: e.tensor_tensor(rstdS, self.ps[bq], rstdS, ALU.subtract),
                   reads=[self.P(bq)] + self.A(rstdS), writes=self.A(rstdS))
            pl.add("act", lambda e: e.activation(rstdS, rstdS, AF.Sqrt, bias=epsc),
                   reads=self.A(rstdS) + [self.R("vecs")], writes=self.A(rstdS))
            pl.add("dve", lambda e: e.reciprocal(rstdS, rstdS), reads=self.A(rstdS), writes=self.A(rstdS))
            for j in range(8):
                gcol = self.vecs[:, vb + V_CLG + j: vb + V_CLG + j + 1]
                bcol = self.vecs[:, vb + V_CLB + j: vb + V_CLB + j + 1]
                src = self.uT[:, j, us]
                dst = self.cnT[:, j, ts]
                pl.add("dve", lambda e, src=src: e.tensor_tensor(tmp, src, meanS, ALU.subtract),
                       reads=self.A(src, meanS), writes=self.A(tmp))
                pl.add("dve", lambda e: e.tensor_tensor(tmp, tmp, rstdS, ALU.mult), reads=self.A(tmp, rstdS), writes=self.A(tmp))
                pl.add("act", lambda e, dst=dst, gcol=gcol, bcol=bcol: e.activation(dst, tmp, AF.Silu, bias=bcol, scale=gcol),
                       reads=self.A(tmp) + [self.R("vecs")], writes=self.A(dst))

    def phase_attn(self, L):
        pl = self.pl
        NB = S // 128
        QT = self.sb("QT", [128, S], BF16, self.o_r1)
        KT = self.sb("KT", [128, S], BF16, self.o_r1 + S * 2)
        Vt = self.sb("Vt", [128, NB, HD], BF16, self.o_w)
        ptile = [self.sb(f"pt{i}", [128, 512], BF16, self.o_w + 2 * SLAB_E * 2 + i * 1024) for i in range(3)]
        LTb = self.misc("LTb", [8, NCORES, 128], F32, 0)
        Lcol = self.misc("Lcol", [128, NB], F32, 4096)
        negC = self.misc("negC", [128, NB], F32, 4096 + 256)
        Cend = self.misc("Cend", [128, NB], F32, 4096 + 512)
        sc0 = self.misc("sc0", [128, NB], F32, 4096 + 768)
        sc1 = self.misc("sc1", [128, NB], F32, 4096 + 1024)
        excl = self.misc("excl", [128, NB], F32, 4096 + 1280)
        biasq = self.misc("biasq", [128, 2, 4, NB], F32, 6144)
        rinv = self.sb("rinv", [128, 512], F32, self.o_w + 2 * SLAB_E * 2 + 3072)
        ot = [self.sb(f"ot{i}", [128, 512], BF16, self.o_w + 2 * SLAB_E * 2 + 5120 + i * 1024) for i in range(2)]
        ident = self.cf[:, CF_ID, :]

        def dyn_load(dst, mk):
            def f(e):
                src = mk(self.pid["v"])
                try:
                    return e.dma_start(out=dst, in_=src)
                except Exception:
                    print("DYNLOAD FAIL dst", dst, "src", src)
                    raise
            return f
        for gat, nm, dstT in ((self.gatk, "gatk", KT), (self.gatq, "gatq", QT)):
            dst = dstT.rearrange("p (r t) -> p r t", r=NCORES)
            mk = lambda pid, gat=gat: gat.rearrange("(r x) t -> x r t", r=NCORES)[bass.ds(pid * HD, HD), :, :]
            pl.add("pool", dyn_load(dst, mk), reads=[self.R(nm)], writes=self.A(dstT), dma=True)
        for r in range(NCORES):
            dst = Vt[:, r * 8:(r + 1) * 8, :]
            mk = lambda pid, r=r: self.gatv.rearrange("(r x) d -> r x d", r=NCORES)[r, bass.ds(pid * T, T), :].rearrange("(b p) d -> p b d", p=128)
            pl.add("pool", dyn_load(dst, mk), reads=[self.R("gatv")], writes=self.A(dst), dma=True)
        mk = lambda pid: self.gatf.rearrange("(r x) p -> x r p", r=NCORES)[bass.ds(pid * 8, 8), :, :]
        pl.add("pool", dyn_load(LTb, mk), reads=[self.R("gatf")], writes=self.A(LTb), dma=True)

        b0, b1, b2 = 7, 6, 5
        for r in range(NCORES):
            pl.add("pe", lambda e, r=r: e.transpose(self.ps[b0][:, r * 8:(r + 1) * 8], LTb[:, r, :], ident[0:8, 0:8]),
                   reads=self.A(LTb) + [self.R("cf")], writes=[self.P(b0)], acc=(r > 0))
        pl.add("act", lambda e: e.copy(Lcol, self.ps[b0][:, 0:NB]), reads=[self.P(b0)], writes=self.A(Lcol))
        self.mm(b1, self.ps[b1][:, 0:NB], self.cf[:, CF_TRIU, :], Lcol, True, True, [self.R("cf")])
        self.mm(b2, self.ps[b2][:, 0:NB], self.cf[:, CF_ONE, :], Lcol, True, True, [self.R("cf")])
        pl.add("act", lambda e: e.copy(sc0, self.ps[b2][:, 0:NB]), reads=[self.P(b2)], writes=self.A(sc0))
        cur, nxt = sc0, sc1
        for sh in (1, 2, 4, 8, 16, 32):
            pl.add("dve", lambda e, cur=cur, nxt=nxt: e.tensor_copy(nxt, cur), reads=self.A(cur), writes=self.A(nxt))
            pl.add("dve", lambda e, cur=cur, nxt=nxt, sh=sh: e.tensor_tensor(nxt[:, sh:NB], nxt[:, sh:NB], cur[:, 0:NB - sh], ALU.add),
                   reads=self.A(cur, nxt), writes=self.A(nxt))
            cur, nxt = nxt, cur
        pl.add("dve", lambda e, cur=cur: e.tensor_copy(Cend, cur), reads=self.A(cur), writes=self.A(Cend))
        pl.add("dve", lambda e: e.memset(excl, 0.0), writes=self.A(excl))
        pl.add("dve", lambda e: e.tensor_copy(excl[:, 1:NB], Cend[:, 0:NB - 1]), reads=self.A(Cend, excl), writes=self.A(excl))
        pl.add("dve", lambda e: e.tensor_tensor(negC, self.ps[b1][:, 0:NB], excl, ALU.add), reads=[self.P(b1)] + self.A(excl), writes=self.A(negC))
        pl.add("dve", lambda e: e.tensor_scalar(negC, negC, -1.0, None, ALU.mult), reads=self.A(negC), writes=self.A(negC))
        if self.stage == "attn" and self.dbg[0] == "negC":
            self.dump(negC, self.A(negC))
            return

        s2 = []
        tiles = []
        for c in range(S // 512):
            for j in range(4 * c + 4):
                tiles.append((c, j))
        LA = 2

        def emit_bias(c):
            bsl = c % 2
            for pp in range(2):
                i = c * 4 + 2 * pp + 1
                dst = biasq[:, bsl, pp, 0:i + 1]
                pl.add("dve", lambda e, dst=dst, i=i: e.tensor_scalar(dst, negC[:, 0:i + 1], Cend[:, i:i + 1], None, ALU.add),
                       reads=self.A(negC, Cend), writes=self.A(dst))

        def emit_S(n):
            c, j = tiles[n]
            if j == 0:
                emit_bias(c)
            m = max(0, j - 4 * c)
            c0 = m * 128
            bs = n % 3
            kk = KT[:, j * 128:(j + 1) * 128]
            qq = QT[:, c * 512 + c0:(c + 1) * 512]
            pl.add("pe", lambda e, bs=bs, c0=c0, kk=kk, qq=qq: e.matmul(self.ps[bs][:, c0:512], kk, qq, start=True, stop=True),
                   reads=self.A(kk, qq), writes=[self.P(bs)])

        def emit_rest(n):
            c, j = tiles[n]
            nkb = 4 * c + 4
            bo, bl = 3 + (c % 2), 5 + (c % 2)
            bsl = c % 2
            m = max(0, j - 4 * c)
            c0 = m * 128
            bs = n % 3
            pt = ptile[n % 3]
            for pp in range(2):
                lo, hi = max(2 * pp, m), 2 * pp + 1
                if lo > hi:
                    continue
                cs = slice(lo * 128, (hi + 1) * 128)
                bia = biasq[:, bsl, pp, j:j + 1]
                pl.add("act", lambda e, bs=bs, pt=pt, cs=cs, bia=bia: e.activation(pt[:, cs], self.ps[bs][:, cs], AF.Exp, bias=bia, scale=SCALE),
                       reads=[self.P(bs)] + self.A(bia), writes=self.A(pt[:, cs]))
            if j >= 4 * c:
                cs = slice(m * 128, (m + 1) * 128)
                pl.add("dve", lambda e, pt=pt, cs=cs: e.tensor_tensor(pt[:, cs], pt[:, cs], self.cb[:, CB_MASK, :], ALU.mult),
                       reads=self.A(pt[:, cs]) + [self.R("cb")], writes=self.A(pt[:, cs]))
            first, last = (j == 0), (j == nkb - 1)
            self.mm(bo, self.ps[bo][:, c0:512], Vt[:, j, :], pt[:, c0:512], first, last)
            self.mm(bl, self.ps[bl][:, c0:512], self.cb[:, CB_ONE, :], pt[:, c0:512], first, last, [self.R("cb")])
            if last:
                pl.add("dve", lambda e, bl=bl: e.reciprocal(rinv, self.ps[bl]), reads=[self.P(bl)], writes=self.A(rinv))
                o = ot[c % 2]
                pl.add("dve", lambda e, bo=bo, o=o: e.tensor_tensor(o, self.ps[bo], rinv, ALU.mult),
                       reads=[self.P(bo)] + self.A(rinv), writes=self.A(o))
                r = self.R("snd2", c)
                s2.append(r)
                dstd = self.snd2[:, c * 512:(c + 1) * 512]
                pl.add("sp", lambda e, dstd=dstd, o=o: e.dma_start(out=dstd, in_=o), reads=self.A(o), writes=[r], dma=True)

        for n in range(min(LA, len(tiles))):
            emit_S(n)
        for n in range(len(tiles)):
            if n + LA < len(tiles):
                emit_S(n + LA)
            emit_rest(n)
        rg = [list(range(NCORES))]
        pl.add("pool", lambda e: e.collective_compute("AllGather", ALU.bypass, replica_groups=rg,
                                                      ins=[self.snd2], outs=[self.gat2]),
               reads=s2, writes=[self.R("gat2")], cc="gat2")

    def phase_mix(self, L):
        pl = self.pl
        win = self.w_in[L]
        attnT = self.sb("attnT", [128, NH, T], BF16, self.o_r1)
        mT = self.sb("mT", [128, 8, T], BF16, self.o_r1 + NH * T * 2)
        sgc = [self.misc(f"sgc{i}", [128, 512], F32, i * 2048) for i in range(2)]
        sga = [self.misc(f"sga{i}", [128, 512], F32, 4096 + i * 2048) for i in range(2)]
        for h in range(NH):
            dst = attnT[:, h, :]

            def f(e, dst=dst, h=h):
                pid = self.pid["v"]
                return e.dma_start(out=dst, in_=self.gat2[h * HD:(h + 1) * HD, bass.ds(pid * T, T)])
            pl.add("pool", f, reads=[self.R("gat2")], writes=self.A(dst), dma=True)
        n = 0
        for ps_ in range(2):
            for fg in range(4):
                f0 = (ps_ * 4 + fg) * 256
                slot = self.wnext % NSLAB
                self.wnext += 1
                wco = self.wload(self.w_co[L][:, f0:f0 + 256], 8, 256, 0, slot)
                wao = self.wload(self.w_ao[L][:, f0:f0 + 256], 8, 256, 2048, slot)
                wgc = self.wload(win[:, C_GC + f0:C_GC + f0 + 256], KD, 256)
                wga = self.wload(win[:, C_GA + f0:C_GA + f0 + 256], KD, 256)
                for fb in range(2):
                    cs = slice(fb * 128, (fb + 1) * 128)
                    kk = fg * 2 + fb
                    for half in range(2):
                        ts = slice(half * 512, (half + 1) * 512)
                        b1, b2, b3, b4 = self.psum(), self.psum(), self.psum(), self.psum()
                        for k in range(KD):
                            self.mm(b2, self.ps[b2], wgc[:, k, cs], self.hT[:, k, ts], k == 0, k == KD - 1)
                        for k in range(8):
                            self.mm(b1, self.ps[b1], wco[:, k, cs], self.cnT[:, k, ts], k == 0, k == 7)
                        for k in range(KD):
                            self.mm(b4, self.ps[b4], wga[:, k, cs], self.hT[:, k, ts], k == 0, k == KD - 1)
                        for k in range(8):
                            self.mm(b3, self.ps[b3], wao[:, k, cs], attnT[:, k, ts], k == 0, k == 7)
                        sc, sa = sgc[n % 2], sga[n % 2]
                        n += 1
                        pl.add("act", lambda e, b2=b2, sc=sc: e.activation(sc, self.ps[b2], AF.Sigmoid), reads=[self.P(b2)], writes=self.A(sc))
                        pl.add("act", lambda e, b4=b4, sa=sa: e.activation(sa, self.ps[b4], AF.Sigmoid), reads=[self.P(b4)], writes=self.A(sa))
                        pl.add("dve", lambda e, b1=b1, sc=sc: e.tensor_tensor(sc, self.ps[b1], sc, ALU.mult), reads=[self.P(b1)] + self.A(sc), writes=self.A(sc))
                        pl.add("dve", lambda e, b3=b3, sa=sa: e.tensor_tensor(sa, self.ps[b3], sa, ALU.mult), reads=[self.P(b3)] + self.A(sa), writes=self.A(sa))
                        dst = mT[:, kk, ts]
                        pl.add("dve", lambda e, sc=sc, sa=sa, dst=dst: e.tensor_tensor(dst, sc, sa, ALU.add), reads=self.A(sc, sa), writes=self.A(dst))
            for og in range(8):
                wo = self.wload(self.w_o[L][ps_ * 1024:(ps_ + 1) * 1024, og * 256:(og + 1) * 256], 8, 256)

                def consume(cb_, half, b, og=og):
                    dst = self.xT[:, og * 2 + cb_, half * 512:(half + 1) * 512]
                    pl.add("dve", lambda e, b=b, dst=dst: e.tensor_tensor(dst, dst, self.ps[b], ALU.add),
                           reads=[self.P(b)] + self.A(dst), writes=self.A(dst))
                self.proj_T(wo, 2, mT, 8, consume)

    def phase_ffn(self, L):
        pl = self.pl
        actT = self.sb("actT", [128, 22, T], BF16, self.o_r1)
        sil = [self.misc(f"sil{i}", [128, 512], F32, i * 2048) for i in range(2)]
        n = 0
        for hh in range(2):
            for ng in range(11):
                c0 = hh * 2816 + ng * 256
                wg = self.wload(self.w_gu[L][:, c0:c0 + 256], KD, 256)
                wu = self.wload(self.w_gu[L][:, FF + c0:FF + c0 + 256], KD, 256)
                for nb in range(2):
                    cs = slice(nb * 128, (nb + 1) * 128)
                    for half in range(2):
                        ts = slice(half * 512, (half + 1) * 512)
                        bg, bu = self.psum(), self.psum()
                        for k in range(KD):
                            self.mm(bg, self.ps[bg], wg[:, k, cs], self.hT[:, k, ts], k == 0, k == KD - 1)
                        for k in range(KD):
                            self.mm(bu, self.ps[bu], wu[:, k, cs], self.hT[:, k, ts], k == 0, k == KD - 1)
                        sl = sil[n % 2]
                        n += 1
                        pl.add("act", lambda e, bg=bg, sl=sl: e.activation(sl, self.ps[bg], AF.Silu), reads=[self.P(bg)], writes=self.A(sl))
                        dst = actT[:, ng * 2 + nb, ts]
                        pl.add("dve", lambda e, bu=bu, sl=sl, dst=dst: e.tensor_tensor(dst, self.ps[bu], sl, ALU.mult),
                               reads=[self.P(bu)] + self.A(sl), writes=self.A(dst))
            for f in range(KD):
                wd = self.wload(self.w_dn[L][hh * 2816:(hh + 1) * 2816, f * 128:(f + 1) * 128], 22, 128)

                def consume(cb_, half, b, f=f):
                    dst = self.xT[:, f, half * 512:(half + 1) * 512]
                    pl.add("dve", lambda e, b=b, dst=dst: e.tensor_tensor(dst, dst, self.ps[b], ALU.add),
                           reads=[self.P(b)] + self.A(dst), writes=self.A(dst))
                self.proj_T(wd, 1, actT, 22, consume)

    def phase_ple(self, L):
        pl = self.pl
        pT = self.sb("pT", [128, 2, T], BF16, self.o_r1)
        pst = [self.sb(f"pst{i}", [128, PLE], F32, self.o_r1 + 4096 + i * 1024) for i in range(2)]
        sg = [self.misc(f"psg{i}", [128, 512], F32, i * 2048) for i in range(2)]
        ident = self.cf[:, CF_ID, :]
        n = 0
        for tt in range(T // 128):
            st = pst[tt % 2]
            pl.add("sp", lambda e, st=st, tt=tt: e.dma_start(out=st, in_=self.p_in[L][tt * 128:(tt + 1) * 128, :]),
                   writes=self.A(st), dma=True)
            b = self.psum()
            for kk in range(2):
                src = st[:, kk * 128:(kk + 1) * 128]
                pl.add("pe", lambda e, b=b, kk=kk, src=src: e.transpose(self.ps[b][:, kk * 128:(kk + 1) * 128], src, ident),
                       reads=self.A(src) + [self.R("cf")], writes=[self.P(b)], acc=(kk > 0))
            for kk in range(2):
                self.evac(n, pT[:, kk, tt * 128:(tt + 1) * 128], b, self.ps[b][:, kk * 128:(kk + 1) * 128])
                n += 1
        wpp = self.sb("wpp", [128, 2, D], BF16, self.o_cn)
        srcw = self.w_pp[L].rearrange("(k p) c -> p k c", p=128)
        pl.add("pool", lambda e: e.dma_start(out=wpp, in_=srcw), writes=self.A(wpp), dma=True)
        n = 0
        for og in range(8):
            wg = self.wload(self.w_pg[L][:, og * 256:(og + 1) * 256], KD, 256)
            for fb in range(2):
                f = og * 2 + fb
                cs = slice(fb * 128, (fb + 1) * 128)
                for half in range(2):
                    ts = slice(half * 512, (half + 1) * 512)
                    bg, bp = self.psum(), self.psum()
                    for k in range(KD):
                        self.mm(bg, self.ps[bg], wg[:, k, cs], self.hT[:, k, ts], k == 0, k == KD - 1)
                    for k in range(2):
                        self.mm(bp, self.ps[bp], wpp[:, k, f * 128:(f + 1) * 128], pT[:, k, ts], k == 0, k == 1)
                    s_ = sg[n % 2]
                    n += 1
                    pl.add("act", lambda e, bg=bg, s_=s_: e.activation(s_, self.ps[bg], AF.Sigmoid), reads=[self.P(bg)], writes=self.A(s_))
                    pl.add("dve", lambda e, bp=bp, s_=s_: e.tensor_tensor(s_, self.ps[bp], s_, ALU.mult), reads=[self.P(bp)] + self.A(s_), writes=self.A(s_))
                    dst = self.xT[:, f, ts]
                    pl.add("dve", lambda e, s_=s_, dst=dst: e.tensor_tensor(dst, dst, s_, ALU.add), reads=self.A(dst, s_), writes=self.A(dst))

    def phase_final(self):
        pl = self.pl
        sq = self.misc("fsq", [128, 2, 512], F32, 0)
        rstd = self.misc("frstd", [128, T], F32, 4096)
        md = self.cf[:, CF_MD, :]
        ident = self.cf[:, CF_ID, :]
        epsc = self.vecs[:, V_EPS:V_EPS + 1]
        for half in range(2):
            ts = slice(half * 512, (half + 1) * 512)
            b = self.psum()
            for k in range(KD):
                sl = k % 2
                src = self.xT[:, k, ts]
                pl.add("act", lambda e, src=src, sl=sl: e.activation(sq[:, sl, :], src, AF.Square), reads=self.A(src), writes=self.A(sq[:, sl, :]))
                self.mm(b, self.ps[b], md, sq[:, sl, :], k == 0, k == KD - 1, [self.R("cf")])
            r = rstd[:, ts]
            pl.add("act", lambda e, b=b, r=r: e.activation(r, self.ps[b], AF.Sqrt, bias=epsc), reads=[self.P(b), self.R("vecs")], writes=self.A(r))
            pl.add("dve", lambda e, r=r: e.reciprocal(r, r), reads=self.A(r), writes=self.A(r))
            for k in range(KD):
                g = self.vecs[:, V_FIN + k:V_FIN + k + 1]
                src = self.xT[:, k, ts]
                pl.add("dve", lambda e, src=src, g=g, r=r: e.scalar_tensor_tensor(src, src, g, r, ALU.mult, ALU.mult),
                       reads=self.A(src, r) + [self.R("vecs")], writes=self.A(src))
        ostg = [self.sb(f"ostg{i}", [128, D], F32, self.o_r1 + i * 8192) for i in range(2)]
        n = 0
        for tt in range(T // 128):
            o = ostg[tt % 2]
            for kg in range(KD // 4):
                b = self.psum()
                for kk in range(4):
                    src = self.xT[:, kg * 4 + kk, tt * 128:(tt + 1) * 128]
                    pl.add("pe", lambda e, b=b, kk=kk, src=src: e.transpose(self.ps[b][:, kk * 128:(kk + 1) * 128], src, ident),
                           reads=self.A(src) + [self.R("cf")], writes=[self.P(b)], acc=(kk > 0))
                self.evac(n, o[:, kg * 512:(kg + 1) * 512], b)
                n += 1
            pl.add("sp", lambda e, o=o, tt=tt: e.dma_start(out=self.out[tt * 128:(tt + 1) * 128, :], in_=o),
                   reads=self.A(o), writes=[self.R("out", tt)], dma=True)

    def dump(self, src_ap, reads):
        self.pl.add("sp", lambda e: e.dma_start(out=self.dbg_out, in_=src_ap), reads=reads, dma=True)

    def dump_xT(self):
        self.dump(self.xT, self.A(self.xT))

    def build(self):
        self.declare()
        self.load_consts()
        self.phase_load_x()
        if self.stage == "loadx":
            self.dump_xT()
            return self.finish()
        layers = list(range(DEPTH)) if self.stage == "full" else ([1] if self.stage == "l1only" else [0])
        for L in layers:
            vb = L * V_LAYER
            self.phase_rmsnorm(vb + V_NMG)
            if self.stage == "norm0":
                self.dump(self.hT, self.A(self.hT))
                return self.finish()
            self.phase_qkvf(L)
            self.phase_glu(L)
            self.phase_halo(L)
            self.phase_conv(L)
            if self.stage == "projA":
                which = self.dbg[0]
                if which == "gatf":
                    self.dump(self.gatf, [self.R("gatf")])
                elif which == "cnT":
                    self.dump(self.cnT, self.A(self.cnT))
                elif which == "uT":
                    self.dump(self.uT, self.A(self.uT))
                return self.finish()
            self.phase_attn(L)
            if self.stage == "attn":
                if self.dbg[0] == "gat2":
                    self.dump(self.gat2, [self.R("gat2")])
                return self.finish()
            self.phase_mix(L)
            if self.stage == "mix":
                self.dump_xT()
                return self.finish()
            self.phase_rmsnorm(vb + V_NFG)
            self.phase_ffn(L)
            if self.stage == "ffn":
                self.dump_xT()
                return self.finish()
            self.phase_rmsnorm(vb + V_NPG)
            self.phase_ple(L)
            if self.stage == "layer0":
                self.dump_xT()
                return self.finish()
        self.phase_final()
        return self.finish()

    def finish(self):
        def pre(e, handle):
            if e == "pool":
                self.pid["v"] = handle.partition_id()
        self.pl.emit(self.nc, pre=pre)
        return self.nc


def _col(v, nchunk):
    return np.ascontiguousarray(np.asarray(v, dtype=np.float32).reshape(nchunk, 128).T)


def make_vecs(inp):
    vecs = np.zeros((128, NV), dtype=np.float32)
    for i in range(DEPTH):
        o = i * V_LAYER
        vecs[:, o + V_NMG:o + V_NMG + 16] = _col(inp["norm_mix_g"][i], 16)
        vecs[:, o + V_NFG:o + V_NFG + 16] = _col(inp["norm_ffn_g"][i], 16)
        vecs[:, o + V_NPG:o + V_NPG + 16] = _col(inp["norm_ple_g"][i], 16)
        vecs[:, o + V_CB:o + V_CB + 8] = _col(inp["conv_b"][i], 8)
        vecs[:, o + V_CLG:o + V_CLG + 8] = _col(inp["conv_ln_g"][i], 8)
        vecs[:, o + V_CLB:o + V_CLB + 8] = _col(inp["conv_ln_b"][i], 8)
        cw = np.asarray(inp["conv_w"][i], dtype=np.float32)
        for k in range(CK):
            vecs[:, o + V_CW + k * 8:o + V_CW + (k + 1) * 8] = _col(cw[k], 8)
        vecs[0:NH, o + V_BF] = np.asarray(inp["b_forget"][i], dtype=np.float32)
    vecs[:, V_FIN:V_FIN + 16] = _col(inp["final_g"], 16)
    vecs[:, V_EPS] = EPS
    return vecs


def make_consts():
    cf = np.zeros((128, 7, 128), dtype=np.float32)
    idx = np.arange(128)
    cf[:, CF_ID, :] = np.eye(128, dtype=np.float32)
    cf[:, CF_MD, :] = 1.0 / D
    cf[:, CF_MC, :] = 1.0 / CC
    cf[:, CF_ONE, :] = 1.0
    cf[:, CF_TRIU, :] = (idx[:, None] <= idx[None, :]).astype(np.float32)
    cf[:, CF_STRICT, :] = (idx[:, None] < idx[None, :]).astype(np.float32)
    cf[:, CF_INCL, :] = (idx[:, None] <= idx[None, :]).astype(np.float32)
    cb = np.zeros((128, 3, 128), dtype=np.float32)
    cb[:, CB_ID, :] = np.eye(128)
    cb[:, CB_ONE, :] = 1.0
    cb[:, CB_MASK, :] = (idx[None, :] >= idx[:, None]).astype(np.float32)
    return cf, cb.astype(ml_dtypes.bfloat16)


def make_in_maps(inp):
    vecs = make_vecs(inp)
    cf, cb = make_consts()
    f = lambda a: np.ascontiguousarray(np.asarray(a, dtype=np.float32))
    shared = {
        "w_in": f(inp["w_in"]), "w_conv_out": f(inp["w_conv_out"]), "w_attn_out": f(inp["w_attn_out"]),
        "w_out": f(inp["w_out"]), "w_gate_up": f(inp["w_gate_up"]), "w_down": f(inp["w_down"]),
        "w_ple_gate": f(inp["w_ple_gate"]), "w_ple_proj": f(inp["w_ple_proj"]),
        "vecs": vecs, "cf": cf, "cb": cb,
    }
    x = f(inp["x"])[0]
    p = f(inp["p"])[:, 0]
    maps = []
    for c in range(NCORES):
        m = dict(shared)
        m["x"] = np.ascontiguousarray(x[c * T:(c + 1) * T])
        m["p"] = np.ascontiguousarray(p[:, c * T:(c + 1) * T])
        maps.append(m)
    return maps


def run(inp, stage="full", dbg=None, trace=False):
    b = Builder(stage=stage, dbg=dbg)
    nc = b.build()
    maps = make_in_maps(inp)
    names = set()
    for alloc in nc.allocations:
        if isinstance(alloc, mybir.MemoryLocationSet) and alloc.kind == "ExternalInput":
            names.add(alloc.memorylocations[0].name)
    maps = [{k: v for k, v in m.items() if k in names} for m in maps]
    res = run_bass_kernel_spmd(nc, maps, core_ids=list(range(NCORES)), trace=trace)
    return res


def kernel(**inputs):
    res = run(inputs, stage="full")
    out = np.concatenate([np.asarray(r["out"]) for r in res.results], axis=0)
    return out.reshape(1, S, D).astype(np.float32)
```
